# Optimizing a Trainium2 kernel written in Bass

```python
import jax, jax.numpy as jnp
from jax import lax
import numpy as np

D_MODEL = 1024
BATCH = 2
SEQ = 8192
DEPTH = 1
DEC_BATCH = 8
DEC_SEQ = 32
PAST_LEN = 4096

CHUNK = 64
D_MIX = D_MODEL
D_CONV = D_MIX // 2
D_RET = D_MIX - D_CONV
N_RET_HEADS = 4
HEAD_DIM = D_RET // N_RET_HEADS
CONV_WIDTH = 31
D_FF = 2816
ROPE_THETA = 10000.0
EPS = 1e-6
D_IN = 2 * D_CONV + 4 * D_RET

kernel_name = "hybrid_conformer_retention_stream_step"


def rmsnorm(x, g):
    xf = x.astype(jnp.float32)
    y = xf * lax.rsqrt(jnp.mean(xf * xf, axis=-1, keepdims=True) + EPS)
    return (y * g.astype(jnp.float32)).astype(x.dtype)


def layernorm(x, g, b):
    xf = x.astype(jnp.float32)
    mu = jnp.mean(xf, axis=-1, keepdims=True)
    var = jnp.mean(jnp.square(xf - mu), axis=-1, keepdims=True)
    y = (xf - mu) * lax.rsqrt(var + EPS)
    return (y * g.astype(jnp.float32) + b.astype(jnp.float32)).astype(x.dtype)


def swiglu(x, w_gate, w_up, w_down):
    return (jax.nn.silu(x @ w_gate) * (x @ w_up)) @ w_down


def rope(x, pos):
    half = HEAD_DIM // 2
    inv = ROPE_THETA ** (-jnp.arange(half, dtype=jnp.float32) / half)
    ang = pos.astype(jnp.float32)[:, None] * inv[None, :]
    cos, sin = jnp.cos(ang), jnp.sin(ang)
    x1 = x[..., :half].astype(jnp.float32)
    x2 = x[..., half:].astype(jnp.float32)
    out = jnp.concatenate([x1 * cos - x2 * sin, x2 * cos + x1 * sin], axis=-1)
    return out.astype(x.dtype)


def retention(q, k, v, R0):
    B, H, T, _ = q.shape
    C = min(T, CHUNK)
    n = T // C
    dt = q.dtype
    log_gamma = jnp.log1p(-jnp.exp2(-5.0 - jnp.arange(H, dtype=jnp.float32)))
    idx = jnp.arange(C, dtype=jnp.float32)
    rel = idx[:, None] - idx[None, :]
    decay_mask = jnp.where(rel >= 0,
                           jnp.exp(log_gamma[:, None, None] * jnp.maximum(rel, 0.0)),
                           0.0).astype(dt)
    xi = jnp.exp(log_gamma[:, None] * (idx + 1.0))[..., None].astype(dt)
    zeta = jnp.exp(log_gamma[:, None] * (C - 1.0 - idx))[..., None].astype(dt)
    g_chunk = jnp.exp(log_gamma * C)[:, None, None].astype(dt)

    def step(R, qkv):
        qc, kc, vc = qkv
        s = jnp.einsum('bhid,bhjd->bhij', qc, kc) * decay_mask
        o = (jnp.einsum('bhij,bhjv->bhiv', s, vc)
             + jnp.einsum('bhid,bhdv->bhiv', qc, R) * xi)
        R = R * g_chunk + jnp.einsum('bhjd,bhjv->bhdv', kc * zeta, vc)
        return R, o

    def split(t):
        return t.reshape(B, H, n, C, t.shape[-1]).transpose(2, 0, 1, 3, 4)

    R, o = lax.scan(step, R0.astype(dt), (split(q), split(k), split(v)))
    o = o.transpose(1, 2, 0, 3, 4).reshape(B, H, T, -1)
    return o, R


def mixer(h, conv_hist, R0, pos0, w_in, conv_w, conv_b, conv_ln_g, conv_ln_b,
          ret_gn_g, w_out):
    B, T, _ = h.shape
    p = h @ w_in
    a, b, q, k, v, g = jnp.split(
        p, [D_CONV, 2 * D_CONV, 2 * D_CONV + D_RET, 2 * D_CONV + 2 * D_RET,
            2 * D_CONV + 3 * D_RET], axis=-1)

    u = a * jax.nn.sigmoid(b)
    u_ext = jnp.concatenate([conv_hist.astype(u.dtype), u], axis=1)
    c = lax.conv_general_dilated(
        u_ext, conv_w[:, None, :].astype(u.dtype), window_strides=(1,), padding='VALID',
        dimension_numbers=('NWC', 'WIO', 'NWC'), feature_group_count=D_CONV) + conv_b
    new_hist = u_ext[:, -(CONV_WIDTH - 1):]
    c = jax.nn.silu(layernorm(c, conv_ln_g, conv_ln_b))

    heads = lambda t: t.reshape(B, T, N_RET_HEADS, HEAD_DIM).transpose(0, 2, 1, 3)
    pos = pos0 + jnp.arange(T)
    qh = rope(heads(q), pos)
    kh = rope(heads(k), pos) * (HEAD_DIM ** -0.5)
    o, R = retention(qh, kh, heads(v), R0)
    of = o.astype(jnp.float32)
    mu = jnp.mean(of, axis=-1, keepdims=True)
    var = jnp.mean(jnp.square(of - mu), axis=-1, keepdims=True)
    on = ((of - mu) * lax.rsqrt(var + EPS)).astype(h.dtype)
    r = on.transpose(0, 2, 1, 3).reshape(B, T, D_RET) * ret_gn_g
    r = jax.nn.silu(g) * r

    out = jnp.concatenate([c, r], axis=-1) @ w_out
    return out, new_hist, R


def layer(x, conv_hist, R0, pos0,
          ffn1_norm_pre, ffn1_w_gate, ffn1_w_up, ffn1_w_down, ffn1_norm_post,
          mix_norm_pre, w_in, conv_w, conv_b, conv_ln_g, conv_ln_b, ret_gn_g, w_out,
          mix_norm_post,
          ffn2_norm_pre, ffn2_w_gate, ffn2_w_up, ffn2_w_down, ffn2_norm_post):
    x = x + 0.5 * rmsnorm(swiglu(rmsnorm(x, ffn1_norm_pre), ffn1_w_gate, ffn1_w_up,
                                 ffn1_w_down), ffn1_norm_post)
    m, new_hist, R = mixer(rmsnorm(x, mix_norm_pre), conv_hist, R0, pos0, w_in, conv_w,
                           conv_b, conv_ln_g, conv_ln_b, ret_gn_g, w_out)
    x = x + rmsnorm(m, mix_norm_post)
    x = x + 0.5 * rmsnorm(swiglu(rmsnorm(x, ffn2_norm_pre), ffn2_w_gate, ffn2_w_up,
                                 ffn2_w_down), ffn2_norm_post)
    return x, new_hist, R


def setup_inputs(seed: int = 0) -> dict:
    key = jax.random.key(seed)
    ks = iter(jax.random.split(key, 32))
    nrm = lambda shape, s: jax.random.normal(next(ks), shape, jnp.float32) * s
    gain = lambda shape: 1.0 + nrm(shape, 0.02)
    L = DEPTH
    return {
        "x_prompt": nrm((BATCH, SEQ, D_MODEL), 1.0),
        "x_sample": nrm((DEC_BATCH, DEC_SEQ, D_MODEL), 1.0),
        "state_conv": nrm((L, DEC_BATCH, CONV_WIDTH - 1, D_CONV), 0.5),
        "state_ret": nrm((L, DEC_BATCH, N_RET_HEADS, HEAD_DIM, HEAD_DIM), 0.5),
        "ffn1_norm_pre": gain((L, D_MODEL)),
        "ffn1_w_gate": nrm((L, D_MODEL, D_FF), D_MODEL ** -0.5),
        "ffn1_w_up": nrm((L, D_MODEL, D_FF), D_MODEL ** -0.5),
        "ffn1_w_down": nrm((L, D_FF, D_MODEL), D_FF ** -0.5),
        "ffn1_norm_post": gain((L, D_MODEL)),
        "mix_norm_pre": gain((L, D_MODEL)),
        "w_in": nrm((L, D_MODEL, D_IN), D_MODEL ** -0.5),
        "conv_w": nrm((L, CONV_WIDTH, D_CONV), CONV_WIDTH ** -0.5),
        "conv_b": nrm((L, D_CONV), 0.02),
        "conv_ln_g": gain((L, D_CONV)),
        "conv_ln_b": nrm((L, D_CONV), 0.02),
        "ret_gn_g": gain((L, D_RET)),
        "w_out": nrm((L, D_MIX, D_MODEL), D_MIX ** -0.5),
        "mix_norm_post": gain((L, D_MODEL)),
        "ffn2_norm_pre": gain((L, D_MODEL)),
        "ffn2_w_gate": nrm((L, D_MODEL, D_FF), D_MODEL ** -0.5),
        "ffn2_w_up": nrm((L, D_MODEL, D_FF), D_MODEL ** -0.5),
        "ffn2_w_down": nrm((L, D_FF, D_MODEL), D_FF ** -0.5),
        "ffn2_norm_post": gain((L, D_MODEL)),
    }


def reference(x_prompt, x_sample, state_conv, state_ret,
              ffn1_norm_pre, ffn1_w_gate, ffn1_w_up, ffn1_w_down, ffn1_norm_post,
              mix_norm_pre, w_in, conv_w, conv_b, conv_ln_g, conv_ln_b, ret_gn_g, w_out,
              mix_norm_post,
              ffn2_norm_pre, ffn2_w_gate, ffn2_w_up, ffn2_w_down, ffn2_norm_post):
    yp, ys = x_prompt, x_sample
    Bp = x_prompt.shape[0]
    conv_p, ret_p, conv_s, ret_s = [], [], [], []
    for l in range(DEPTH):
        w = (ffn1_norm_pre[l], ffn1_w_gate[l], ffn1_w_up[l], ffn1_w_down[l],
             ffn1_norm_post[l], mix_norm_pre[l], w_in[l], conv_w[l], conv_b[l],
             conv_ln_g[l], conv_ln_b[l], ret_gn_g[l], w_out[l], mix_norm_post[l],
             ffn2_norm_pre[l], ffn2_w_gate[l], ffn2_w_up[l], ffn2_w_down[l],
             ffn2_norm_post[l])
        hist0 = jnp.zeros((Bp, CONV_WIDTH - 1, D_CONV), yp.dtype)
        R0 = jnp.zeros((Bp, N_RET_HEADS, HEAD_DIM, HEAD_DIM), yp.dtype)
        yp, hp, rp = layer(yp, hist0, R0, 0, *w)
        ys, hs, rs = layer(ys, state_conv[l], state_ret[l], PAST_LEN, *w)
        conv_p.append(hp); ret_p.append(rp); conv_s.append(hs); ret_s.append(rs)
    new_conv_prompt = jnp.stack(conv_p)
    new_ret_prompt = jnp.stack(ret_p)
    new_conv_sample = jnp.stack(conv_s)
    new_ret_sample = jnp.stack(ret_s)
    return (yp, ys, new_conv_prompt, new_ret_prompt, new_conv_sample, new_ret_sample)
```

```python
import numpy as np
from contextlib import ExitStack

import concourse.bass as bass
import concourse.mybir as mybir
from concourse.bass_utils import run_bass_kernel_spmd

F32 = mybir.dt.float32
BF16 = mybir.dt.bfloat16
AF = mybir.ActivationFunctionType
ALU = mybir.AluOpType
AX = mybir.AxisListType

D = 1024
DFF = 2816
NJ = DFF // 128
KC = D // 128
DCONV = 512
DRET = 512
NH = 4
HD = 128
CW = 31
EPS = 1e-6
SEQ = 8192
PAST = 4096
NCORES = 8
SEG = 2048
NBLK = SEG // 128
SB = 64
NTOK = SEG + SB
SKIP_CC = False
CC_DELAY = 0
SELF_SYNC = True
DEBUG_OUT = False
MX_MAXOPS = None
PHASES = ('f1', 'kv', 'ex', 'mx', 'f2')


ENGS = ("pe", "act", "dve", "pool", "sp")


class Prog:
    def __init__(self, nc, stack):
        self.nc = nc
        self.stack = stack
        self.esem = {e: stack.enter_context(nc.semaphore("es_" + e)) for e in ENGS}
        self.ecount = {e: 0 for e in ENGS}
        self.chan = {}
        self.state = {}
        self.ops = []
        self.waited = {e: {} for e in ENGS}
        self.self_sync = SELF_SYNC
        self.maxops = None

    def _chan(self, name):
        if name not in self.chan:
            self.chan[name] = [self.stack.enter_context(self.nc.semaphore("ch_" + name)), 0]
        return self.chan[name]

    def op(self, eng, fn, reads=(), writes=()):
        if self.maxops is not None and len(self.ops) >= self.maxops:
            return None
        deps = self._deps(reads, writes)
        idx = len(self.ops)
        rec = dict(eng=eng, fn=fn, deps=deps, marked=False, kind="c", count=None)
        self.ops.append(rec)
        tok = ("e", idx)
        self._update(tok, reads, writes)
        return rec

    def dma(self, queue, fn, chan, ndma=1, reads=(), writes=(), inc=16):
        if self.maxops is not None and len(self.ops) >= self.maxops:
            return None
        deps = self._deps(reads, writes)
        ch = self._chan(chan)
        ch[1] += inc * ndma
        idx = len(self.ops)
        rec = dict(eng=queue, fn=fn, deps=deps, marked=False, kind="d", chan=chan, count=ch[1])
        self.ops.append(rec)
        self._update(("d", chan, ch[1]), reads, writes)
        return rec

    @staticmethod
    def _is_psum(k):
        n = k if isinstance(k, str) else k[0]
        return isinstance(n, str) and (n.startswith("ps") or n == "bk")

    def _deps(self, reads, writes):
        deps = []
        for k in reads:
            st = self.state.get(k)
            if st and st[0] is not None:
                deps.append(st[0])
            if st and self._is_psum(k):
                deps.extend(st[1])
        for k in writes:
            st = self.state.get(k)
            if st:
                if st[0] is not None:
                    deps.append(st[0])
                deps.extend(st[1])
        return deps

    def _update(self, tok, reads, writes):
        for k in reads:
            st = self.state.setdefault(k, [None, []])
            st[1].append(tok)
        for k in writes:
            self.state[k] = [tok, []]

    def emit(self, final=False):
        nc = self.nc
        ops = self.ops
        self.last_nops = len(ops)
        for rec in ops:
            for d in rec["deps"]:
                if d[0] == "e":
                    src = ops[d[1]] if d[1] < len(ops) and d[1] >= 0 else None
                    if src is None:
                        continue
                    if src["eng"] == rec["eng"] and (src["eng"] == "pe" or not self.self_sync):
                        continue
                    src["marked"] = True
        last = {}
        for i, rec in enumerate(ops):
            if rec["kind"] == "c":
                last[rec["eng"]] = i
        for e, i in last.items():
            ops[i]["marked"] = True
        cnt = dict(self.ecount)
        for rec in ops:
            if rec["kind"] == "c" and rec["marked"]:
                cnt[rec["eng"]] += 1
                rec["count"] = cnt[rec["eng"]]
        end_counts = dict(cnt)
        end_chan = {k: v[1] for k, v in self.chan.items()}

        def waits_for(rec):
            out = []
            for d in rec["deps"]:
                if d[0] == "e":
                    src = ops[d[1]]
                    if src["eng"] == rec["eng"] and (src["eng"] == "pe" or not self.self_sync):
                        continue
                    out.append((self.esem[src["eng"]], src["count"]))
                else:
                    out.append((self.chan[d[1]][0], d[2]))
            return out

        def run_engine(ename, eng):
            wd = self.waited[ename]
            for rec in ops:
                if rec["eng"] != ename:
                    continue
                for sem, val in waits_for(rec):
                    key = id(sem)
                    if wd.get(key, 0) >= val:
                        continue
                    wd[key] = val
                    eng.wait_ge(sem, val)
                if rec["kind"] == "c":
                    ins = rec["fn"](eng)
                    if rec["marked"]:
                        ins.then_inc(self.esem[ename], 1)
                else:
                    rec["fn"](eng, self.chan[rec["chan"]][0])
            for e2 in ENGS:
                if e2 == ename or end_counts[e2] == 0:
                    continue
                key = id(self.esem[e2])
                if wd.get(key, 0) < end_counts[e2]:
                    wd[key] = end_counts[e2]
                    eng.wait_ge(self.esem[e2], end_counts[e2])
            for cname, cval in end_chan.items():
                sem = self.chan[cname][0]
                key = id(sem)
                if cval > 0 and wd.get(key, 0) < cval:
                    wd[key] = cval
                    eng.wait_ge(sem, cval)

        with nc.Block() as block:
            @block.tensor
            def _(eng):
                run_engine("pe", eng)

            @block.scalar
            def _(eng):
                run_engine("act", eng)

            @block.vector
            def _(eng):
                run_engine("dve", eng)

            @block.gpsimd
            def _(eng):
                run_engine("pool", eng)

            @block.sync
            def _(eng):
                run_engine("sp", eng)

        self.ecount = end_counts
        self.ops = []
        self.state = {}


class Builder:
    def __init__(self, cfg):
        self.cfg = cfg
        self.nc = bass.Bass("TRN2", target_bir_lowering=False)
        self.top = ExitStack()
        self.P = Prog(self.nc, self.top)
        self.dr = {}

    def din(self, name, shape, dt=F32):
        t = self.nc.dram_tensor(name, list(shape), dt, kind="ExternalInput")
        self.dr[name] = t
        return t

    def dout(self, name, shape, dt=F32):
        t = self.nc.dram_tensor(name, list(shape), dt, kind="ExternalOutput")
        self.dr[name] = t
        return t

    def dint(self, name, shape, dt=F32):
        t = self.nc.dram_tensor(name, list(shape), dt)
        self.dr[name] = t
        return t

    def sb(self, stack, name, shape, dt):
        return stack.enter_context(self.nc.sbuf_tensor(name, list(shape), dt))

    def ps(self, stack, name, shape, dt=F32):
        return stack.enter_context(self.nc.psum_tensor(name, list(shape), dt))

    def setup_consts(self):
        P, nc = self.P, self.nc
        st = self.top
        self.ident_f = self.sb(st, "ident_f", [128, 128], F32)
        self.ident_b = self.sb(st, "ident_b", [128, 128], BF16)
        idd = self.dr["ident"]
        P.dma("sp", lambda e, s: e.dma_start(out=self.ident_f[:], in_=idd[:, :]).then_inc(s, 16),
              "c_ident", writes=["ident_f"])
        P.op("dve", lambda e: e.tensor_copy(out=self.ident_b[:], in_=self.ident_f[:]),
             reads=["ident_f"], writes=["ident_b"])
        self.eps_t = self.sb(st, "eps_t", [128, 1], F32)
        P.op("dve", lambda e: e.memset(self.eps_t[:], EPS), writes=["eps_t"])

    def load_bcast(self, stack, name, dram_vec, n):
        t = self.sb(stack, name, [128, n], F32)
        src = dram_vec[0:1, :].partition_broadcast(128)
        self.P.dma("sp", lambda e, s: e.dma_start(out=t[:], in_=src).then_inc(s, 16),
                   "c_" + name, writes=[name])
        return t

    def ffn_phase(self, tag, src, dst_fn, wg, wu, wd, gpre, gpost, tiles):
        P, nc = self.P, self.nc
        with ExitStack() as st:
            NBmax = max(len(t) for t in tiles)
            NTmax = max(sum(b[1] for b in t) for t in tiles)
            nsub_max = (NTmax + 511) // 512
            assert nsub_max <= 3
            gpre_b = self.load_bcast(st, tag + "gpre", gpre, D)
            gpost_b = self.load_bcast(st, tag + "gpost", gpost, D)
            P.op("dve", lambda e: e.tensor_scalar(out=gpost_b[:], in0=gpost_b[:], scalar1=0.5, scalar2=None,
                                                  op0=ALU.mult), reads=[tag + "gpost"], writes=[tag + "gpost"])
            xin = [self.sb(st, f"{tag}xin{i}", [128, D], F32) for i in range(2)]
            xres = [self.sb(st, f"{tag}xres{i}", [128, D], F32) for i in range(2)]
            junk = self.sb(st, tag + "junk", [128, D], BF16)
            ss = self.sb(st, tag + "ss", [128, 2 * NBmax + 4], F32)
            rs = self.sb(st, tag + "rs", [128, 2 * NBmax + 4], F32)
            htok = [self.sb(st, f"{tag}htok{i}", [128, D], BF16) for i in range(2)]
            hT = self.sb(st, tag + "hT", [128, KC, NTmax], BF16)
            aT = self.sb(st, tag + "aT", [128, NJ, NTmax], BF16)
            JG = 4
            wgs = [self.sb(st, f"{tag}wgs{i}", [128, KC, JG * 128], BF16) for i in range(2)]
            wus = [self.sb(st, f"{tag}wus{i}", [128, KC, JG * 128], BF16) for i in range(2)]
            wds = self.sb(st, tag + "wds", [128, NJ, D], BF16)
            sg = [self.sb(st, f"{tag}sg{i}", [128, 512], F32) for i in range(2)]
            dsb = [self.sb(st, f"{tag}dsb{i}", [128, D], F32) for i in range(2)]
            xo = [self.sb(st, f"{tag}xo{i}", [128, D], F32) for i in range(2)]
            psT = self.ps(st, tag + "psT", [128, KC, 128], BF16)
            psG = [self.ps(st, f"{tag}psG{i}", [128, 512], F32) for i in range(min(2, nsub_max))]
            psU = [self.ps(st, f"{tag}psU{i}", [128, 512], F32) for i in range(min(2, nsub_max))]
            if nsub_max == 3:
                psX = self.ps(st, f"{tag}psX", [128, 512], F32)
                psG.append(psX[:, 0:256])
                psU.append(psX[:, 256:512])
            psD = self.ps(st, tag + "psD", [128, D], F32)

            wg_v = wg.ap().rearrange("(kc p) c -> p kc c", p=128)
            wu_v = wu.ap().rearrange("(kc p) c -> p kc c", p=128)
            wd_v = wd.ap().rearrange("(j p) c -> p j c", p=128)
            groups = []
            j0 = 0
            while j0 < NJ:
                n = min(JG, NJ - j0)
                groups.append((j0, n))
                j0 += n
            cnt = {"x": 0, "sg": 0, "d": 0, "r": 0, "slab": 0, "h": 0}
            nslab_total = len(tiles) * len(groups)

            def issue_slab():
                k = cnt["slab"]
                if k >= nslab_total:
                    return
                cnt["slab"] += 1
                sl = k % 2
                j0, n = groups[k % len(groups)]
                P.dma("pool", lambda e, s, sl=sl, j0=j0, n=n: e.dma_start(
                    out=wgs[sl][:, :, 0:n * 128], in_=wg_v[:, :, j0 * 128:(j0 + n) * 128]).then_inc(s, 16),
                    f"{tag}wg{sl}", writes=[f"wgs{sl}"])
                P.dma("pool", lambda e, s, sl=sl, j0=j0, n=n: e.dma_start(
                    out=wus[sl][:, :, 0:n * 128], in_=wu_v[:, :, j0 * 128:(j0 + n) * 128]).then_inc(s, 16),
                    f"{tag}wu{sl}", writes=[f"wus{sl}"])

            def load_x(r0, pt):
                sl = cnt["x"] % 2
                cnt["x"] += 1
                xt = xin[sl]
                P.dma("sp", lambda e, s, xt=xt, r0=r0, pt=pt: e.dma_start(
                    out=xt[0:pt, :], in_=src[r0:r0 + pt, :]).then_inc(s, 16),
                    f"{tag}xin{sl}", writes=[f"xin{sl}"])
                return xt, f"xin{sl}"

            def pass_a(ti):
                blks = tiles[ti]
                nb = len(blks)
                c0 = (ti % 2) * NBmax
                if ti < 2:
                    P.op("dve", lambda e, c0=c0: e.memset(ss[:, c0:c0 + NBmax], 1.0), writes=[("ssinit", ti)])
                for b, (r0, pt) in enumerate(blks):
                    xt, xk = load_x(r0, pt)
                    col = c0 + b
                    P.op("act", lambda e, xt=xt, pt=pt, col=col: e.activation(
                        out=junk[0:pt, :], in_=xt[0:pt, :], func=AF.Square,
                        accum_out=ss[0:pt, col:col + 1]),
                        reads=[xk, ("ssinit", ti % 2)], writes=[("ss", col)])
                P.op("act", lambda e, c0=c0, nb=nb: e.activation(
                    out=rs[:, c0:c0 + nb], in_=ss[:, c0:c0 + nb], func=AF.Sqrt,
                    scale=1.0 / D, bias=self.eps_t[:, :]),
                    reads=[("ss", c0 + b) for b in range(nb)] + ["eps_t"], writes=[("rsq", ti % 2)])
                P.op("dve", lambda e, c0=c0, nb=nb: e.reciprocal(
                    out=rs[:, c0:c0 + nb], in_=rs[:, c0:c0 + nb]),
                    reads=[("rsq", ti % 2)], writes=[("rs", ti % 2)])

            pb_state = {}

            def pass_b(ti, b):
                pass_b1(ti, b)
                pass_b2(ti, b)

            def pass_b1(ti, b):
                r0, pt = tiles[ti][b]
                col = (ti % 2) * NBmax + b
                xt, xk = load_x(r0, pt)
                hs = cnt["h"] % 2
                cnt["h"] += 1
                ht = htok[hs]
                pb_state[(ti, b)] = (hs, ht)
                P.op("dve", lambda e, xt=xt, pt=pt, col=col, ht=ht: e.scalar_tensor_tensor(
                    out=ht[0:pt, :], in0=xt[0:pt, :], scalar=rs[0:pt, col:col + 1],
                    in1=gpre_b[0:pt, :], op0=ALU.mult, op1=ALU.mult),
                    reads=[xk, ("rs", ti % 2), tag + "gpre"], writes=[("htok", hs)])

            def pass_b2(ti, b):
                r0, pt = tiles[ti][b]
                cb0 = sum(x[1] for x in tiles[ti][:b])
                hs, ht = pb_state[(ti, b)]

                def tr(e, pt=pt, ht=ht):
                    ins = None
                    for kc in range(KC):
                        ins = e.transpose(out=psT[:, kc, 0:pt], in_=ht[0:pt, kc * 128:(kc + 1) * 128],
                                          identity=self.ident_b[0:pt, 0:pt])
                    return ins
                P.op("pe", tr, reads=[("htok", hs), "ident_b"], writes=["psT"])
                P.op("act", lambda e, cb0=cb0, pt=pt: e.copy(
                    out=hT[:, :, cb0:cb0 + pt], in_=psT[:, :, 0:pt]),
                    reads=["psT"], writes=["hT"])

            def gu(ti, mid_hooks):
                NT = sum(b[1] for b in tiles[ti])
                nsub = (NT + 511) // 512
                for gi, (j0, n) in enumerate(groups):
                    sl = (ti * len(groups) + gi) % 2
                    ng = len(groups)
                    sched = {2: [0], 3: [1], 4: [2, 3], 5: [4, 5]} if ng == 6 else {gi_: [gi_] for gi_ in range(ng)}
                    for wc in sched.get(gi, []):
                        wj0, wn = groups[wc]
                        P.dma("pool", lambda e, s, wj0=wj0, wn=wn: e.dma_start(
                            out=wds[:, wj0:wj0 + wn, :], in_=wd_v[:, wj0:wj0 + wn, :]).then_inc(s, 16),
                            f"{tag}wd{wc}", writes=[("wds", wc)])
                    for jj in range(n):
                        j = j0 + jj
                        for si in range(nsub):
                            c0 = si * 512
                            cn = min(512, NT - c0)

                            def mm(e, w, pst, jj=jj, c0=c0, cn=cn):
                                ins = None
                                for kc in range(KC):
                                    ins = e.matmul(pst[:, 0:cn], lhsT=w[:, kc, jj * 128:(jj + 1) * 128],
                                                   rhs=hT[:, kc, c0:c0 + cn],
                                                   start=(kc == 0), stop=(kc == KC - 1))
                                return ins
                            P.op("pe", lambda e, mm=mm, sl=sl, si=si: mm(e, wgs[sl], psG[si]),
                                 reads=[f"wgs{sl}", "hT"], writes=[("psG", si)])
                            P.op("pe", lambda e, mm=mm, sl=sl, si=si: mm(e, wus[sl], psU[si]),
                                 reads=[f"wus{sl}", "hT"], writes=[("psU", si)])
                            sgi = cnt["sg"] % 2
                            cnt["sg"] += 1
                            P.op("act", lambda e, si=si, cn=cn, sgi=sgi: e.activation(
                                out=sg[sgi][:, 0:cn], in_=psG[si][:, 0:cn], func=AF.Silu),
                                reads=[("psG", si)], writes=[("sg", sgi)])
                            P.op("dve", lambda e, si=si, cn=cn, sgi=sgi, j=j, c0=c0: e.tensor_tensor(
                                out=aT[:, j, c0:c0 + cn], in0=sg[sgi][:, 0:cn], in1=psU[si][:, 0:cn],
                                op=ALU.mult),
                                reads=[("sg", sgi), ("psU", si)], writes=[("aT", j)])
                        for hk in mid_hooks.get(j, []):
                            hk()
                    issue_slab()

            def dn(ti):
                blks = tiles[ti]
                nb = len(blks)
                nxt = ti + 1 if ti + 1 < len(tiles) else None
                for b, (r0, pt) in enumerate(blks):
                    cb0 = sum(x[1] for x in blks[:b])
                    rsl = cnt["r"] % 2
                    cnt["r"] += 1
                    xr = xres[rsl]
                    P.dma("sp", lambda e, s, xr=xr, r0=r0, pt=pt: e.dma_start(
                        out=xr[0:pt, :], in_=src[r0:r0 + pt, :]).then_inc(s, 16),
                        f"{tag}xres{rsl}", writes=[f"xres{rsl}"])

                    col = 2 * NBmax + (cnt["d"] % 2) * 2
                    dsl = cnt["d"] % 2
                    cnt["d"] += 1
                    if nxt is not None and b < len(tiles[nxt]):
                        pass_b1(nxt, b)

                    def mmd(e, hh, cb0=cb0, pt=pt):
                        ins = None
                        for j in range(NJ):
                            ins = e.matmul(psD[0:pt, hh * 512:(hh + 1) * 512],
                                           lhsT=aT[:, j, cb0:cb0 + pt],
                                           rhs=wds[:, j, hh * 512:(hh + 1) * 512],
                                           start=(j == 0), stop=(j == NJ - 1))
                        return ins
                    for hh in range(2):
                        P.op("pe", lambda e, mmd=mmd, hh=hh: mmd(e, hh),
                             reads=[("aT", j) for j in range(NJ)] + [("wds", g) for g in range(len(groups))],
                             writes=[("psD", hh)])
                        P.op("act", lambda e, pt=pt, col=col, hh=hh: e.activation(
                            out=junk[0:pt, 0:512], in_=psD[0:pt, hh * 512:(hh + 1) * 512], func=AF.Square,
                            accum_out=ss[0:pt, col + hh:col + hh + 1]),
                            reads=[("psD", hh)], writes=[("ss", col + hh)])
                        P.op("dve", lambda e, pt=pt, dsl=dsl, hh=hh: e.tensor_copy(
                            out=dsb[dsl][0:pt, hh * 512:(hh + 1) * 512], in_=psD[0:pt, hh * 512:(hh + 1) * 512]),
                            reads=[("psD", hh)], writes=[("draw", dsl, hh)])
                    if nxt is not None and b < len(tiles[nxt]):
                        pass_b2(nxt, b)
                    P.op("dve", lambda e, pt=pt, col=col: e.tensor_tensor(
                        out=ss[0:pt, col:col + 1], in0=ss[0:pt, col:col + 1], in1=ss[0:pt, col + 1:col + 2], op=ALU.add),
                        reads=[("ss", col), ("ss", col + 1)], writes=[("ssum", col)])
                    P.op("act", lambda e, pt=pt, col=col: e.activation(
                        out=rs[0:pt, col:col + 1], in_=ss[0:pt, col:col + 1], func=AF.Sqrt,
                        scale=1.0 / D, bias=self.eps_t[0:pt, :]),
                        reads=[("ssum", col), "eps_t"], writes=[("rsq", col)])
                    P.op("dve", lambda e, pt=pt, col=col: e.reciprocal(
                        out=rs[0:pt, col:col + 1], in_=rs[0:pt, col:col + 1]),
                        reads=[("rsq", col)], writes=[("rs", col)])
                    P.op("dve", lambda e, pt=pt, col=col, dsl=dsl: e.scalar_tensor_tensor(
                        out=dsb[dsl][0:pt, :], in0=dsb[dsl][0:pt, :],
                        scalar=rs[0:pt, col:col + 1], in1=gpost_b[0:pt, :],
                        op0=ALU.mult, op1=ALU.mult),
                        reads=[("draw", dsl, 0), ("draw", dsl, 1), ("rs", col), tag + "gpost"], writes=[("dsb", dsl)])
                    P.op("dve", lambda e, pt=pt, dsl=dsl, xr=xr: e.tensor_tensor(
                        out=xo[dsl][0:pt, :], in0=dsb[dsl][0:pt, :], in1=xr[0:pt, :], op=ALU.add),
                        reads=[("dsb", dsl), f"xres{rsl}"], writes=[("xo", dsl)])
                    o = dst_fn(ti, b)
                    if o is not None:
                        dten, d0, dn_rows = o
                        P.dma("sp", lambda e, s, dsl=dsl, dten=dten, d0=d0, dn_rows=dn_rows: e.dma_start(
                            out=dten[d0:d0 + dn_rows, :], in_=xo[dsl][0:dn_rows, :]).then_inc(s, 16),
                            f"{tag}xo{dsl}", reads=[("xo", dsl)])
                if nxt is not None:
                    for b in range(nb, len(tiles[nxt])):
                        pass_b(nxt, b)

            nt = len(tiles)
            issue_slab()
            issue_slab()
            pass_a(0)
            for b in range(len(tiles[0])):
                pass_b(0, b)
            for ti in range(nt):
                hooks = {}
                if ti + 1 < nt:
                    hooks[7] = [lambda ti=ti: pass_a(ti + 1)]
                gu(ti, hooks)
                dn(ti)
            P.emit()

    def norm_block(self, tag, x_sb, xkey, pt, gb, gkey, htok, hkey, ssc, rsc, sskey, junk):
        P = self.P
        P.op("act", lambda e: e.activation(out=junk[0:pt, :], in_=x_sb[0:pt, :], func=AF.Square,
                                           accum_out=ssc[0:pt, :]),
             reads=[xkey], writes=[sskey])
        P.op("act", lambda e: e.activation(out=rsc[0:pt, :], in_=ssc[0:pt, :], func=AF.Sqrt,
                                           scale=1.0 / D, bias=self.eps_t[0:pt, :]),
             reads=[sskey, "eps_t"], writes=[sskey + "q"])
        P.op("dve", lambda e: e.reciprocal(out=rsc[0:pt, :], in_=rsc[0:pt, :]),
             reads=[sskey + "q"], writes=[sskey + "r"])
        P.op("dve", lambda e: e.scalar_tensor_tensor(out=htok[0:pt, :], in0=x_sb[0:pt, :], scalar=rsc[0:pt, :],
                                                     in1=gb[0:pt, :], op0=ALU.mult, op1=ALU.mult),
             reads=[xkey, sskey + "r", gkey], writes=[hkey])

    def transpose_to(self, in_tile, inkey, pt, nch, psT, out_ap, outkeys, eng="act"):
        P = self.P

        def tr(e):
            ins = None
            for c in range(nch):
                ins = e.transpose(out=psT[:, c, 0:pt], in_=in_tile[0:pt, c * 128:(c + 1) * 128],
                                  identity=self.ident_b[0:pt, 0:pt])
            return ins
        P.op("pe", tr, reads=[inkey, "ident_b"], writes=["psT"])
        if eng == "act":
            P.op("act", lambda e: e.copy(out=out_ap, in_=psT[:, 0:nch, 0:pt]), reads=["psT"], writes=outkeys)
        else:
            P.op("dve", lambda e: e.tensor_copy(out=out_ap, in_=psT[:, 0:nch, 0:pt]), reads=["psT"], writes=outkeys)

    def kv_phase(self, blocks, x1s, ks, vs, w_in, gmix, tkc, tks, Lsend):
        P, nc = self.P, self.nc
        with ExitStack() as st:
            g_b = self.load_bcast(st, "kvg", gmix, D)
            wk = self.sb(st, "kv_wk", [128, KC, 512], BF16)
            wv = self.sb(st, "kv_wv", [128, KC, 512], BF16)
            w_v = w_in.ap().rearrange("(kc p) c -> p kc c", p=128)
            P.dma("pool", lambda e, s: e.dma_start(out=wk[:], in_=w_v[:, :, 1536:2048]).then_inc(s, 16),
                  "kv_wk", writes=["wk"])
            P.dma("pool", lambda e, s: e.dma_start(out=wv[:], in_=w_v[:, :, 2048:2560]).then_inc(s, 16),
                  "kv_wv", writes=["wv"])
            xin = [self.sb(st, f"kv_x{i}", [128, D], F32) for i in range(2)]
            tc = [self.sb(st, f"kv_tc{i}", [128, NH, HD], F32) for i in range(2)]
            ts = [self.sb(st, f"kv_ts{i}", [128, NH, HD], F32) for i in range(2)]
            junk = self.sb(st, "kv_junk", [128, D], BF16)
            ssc = [self.sb(st, f"kv_ss{i}", [128, 1], F32) for i in range(2)]
            rsc = [self.sb(st, f"kv_rs{i}", [128, 1], F32) for i in range(2)]
            htok = [self.sb(st, f"kv_h{i}", [128, D], BF16) for i in range(2)]
            hT = [self.sb(st, f"kv_hT{i}", [128, KC, 128], BF16) for i in range(2)]
            r1 = [self.sb(st, f"kv_r1{i}", [128, NH, HD], F32) for i in range(2)]
            r2 = [self.sb(st, f"kv_r2{i}", [128, NH, HD], F32) for i in range(2)]
            kb = [self.sb(st, f"kv_kb{i}", [128, NH, HD], BF16) for i in range(2)]
            vb = [self.sb(st, f"kv_vb{i}", [128, NH, HD], BF16) for i in range(2)]
            Lf = self.sb(st, "kv_Lf", [128, NH, HD], F32)
            psT = self.ps(st, "kv_psT", [128, KC, 128], BF16)
            psK = [self.ps(st, f"kv_psK{i}", [128, NH, HD], F32) for i in range(2)]
            psV = [self.ps(st, f"kv_psV{i}", [128, NH, HD], F32) for i in range(2)]
            psU = self.ps(st, "kv_psU", [128, NH, HD], F32)
            P.op("dve", lambda e: e.memset(Lf[:], 0.0), writes=["Lf"])
            gcs = self.gc128
            for bi, (r0, pt, chain) in enumerate(blocks):
                s = bi % 2
                P.dma("sp", lambda e, sm, s=s, r0=r0, pt=pt: e.dma_start(
                    out=xin[s][0:pt, :], in_=x1s[r0:r0 + pt, :]).then_inc(sm, 16), f"kv_x{s}", writes=[f"x{s}"])
                P.dma("sp", lambda e, sm, s=s, r0=r0, pt=pt: e.dma_start(
                    out=tc[s][0:pt], in_=tkc[r0:r0 + pt, :].rearrange("p (h d) -> p h d", h=NH)).then_inc(sm, 16),
                    f"kv_tc{s}", writes=[f"tc{s}"])
                P.dma("sp", lambda e, sm, s=s, r0=r0, pt=pt: e.dma_start(
                    out=ts[s][0:pt], in_=tks[r0:r0 + pt, :].rearrange("p (h d) -> p h d", h=NH)).then_inc(sm, 16),
                    f"kv_ts{s}", writes=[f"ts{s}"])
                self.norm_block("kv", xin[s], f"x{s}", pt, g_b, "kvg", htok[s], f"h{s}", ssc[s], rsc[s],
                                f"ss{s}", junk)
                self.transpose_to(htok[s], f"h{s}", pt, KC, psT, hT[s][:, :, 0:pt], [f"hT{s}"])

                def proj(e, w, pst, s=s, pt=pt):
                    ins = None
                    for kc in range(KC):
                        ins = e.matmul(pst[0:pt].rearrange("p h d -> p (h d)"), lhsT=hT[s][:, kc, 0:pt],
                                       rhs=w[:, kc, :], start=(kc == 0), stop=(kc == KC - 1))
                    return ins
                P.op("pe", lambda e, proj=proj, s=s: proj(e, wk, psK[s]), reads=["wk", f"hT{s}"], writes=[f"psK{s}"])
                P.op("pe", lambda e, proj=proj, s=s: proj(e, wv, psV[s]), reads=["wv", f"hT{s}"], writes=[f"psV{s}"])
                self.rope(psK[s], f"psK{s}", tc[s], f"tc{s}", ts[s], f"ts{s}", r1[s], f"r1{s}", r2[s], f"r2{s}",
                          kb[s], f"kb{s}", pt)
                P.op("act", lambda e, s=s, pt=pt: e.copy(out=vb[s][0:pt], in_=psV[s][0:pt]),
                     reads=[f"psV{s}"], writes=[f"vb{s}"])
                P.dma("sp", lambda e, sm, s=s, r0=r0, pt=pt: e.dma_start(
                    out=ks[r0:r0 + pt, :], in_=kb[s][0:pt].rearrange("p h d -> p (h d)")).then_inc(sm, 16),
                    f"kv_ko{s}", reads=[f"kb{s}"])
                P.dma("sp", lambda e, sm, s=s, r0=r0, pt=pt: e.dma_start(
                    out=vs[r0:r0 + pt, :], in_=vb[s][0:pt].rearrange("p h d -> p (h d)")).then_inc(sm, 16),
                    f"kv_vo{s}", reads=[f"vb{s}"])
                if chain:
                    def mu(e, s=s, pt=pt):
                        ins = None
                        for h in range(NH):
                            ins = e.matmul(psU[:, h, :], lhsT=kb[s][0:pt, h, :], rhs=vb[s][0:pt, h, :],
                                           start=True, stop=True)
                        return ins
                    P.op("pe", mu, reads=[f"kb{s}", f"vb{s}"], writes=["psU"])
                    P.op("dve", lambda e: e.tensor_tensor(out=Lf[:], in0=Lf[:], in1=psU[:], op=ALU.add),
                         reads=["Lf", "psU"], writes=["Lf"])
                    P.op("pool", lambda e: e.tensor_tensor(out=Lf[:], in0=Lf[:], in1=gcs[:], op=ALU.mult),
                         reads=["Lf", "gc128"], writes=["Lf"])
            P.dma("sp", lambda e, sm: e.dma_start(out=Lsend[:, :], in_=Lf[:].rearrange("p h d -> p (h d)")).then_inc(sm, 16),
                  "kv_L", reads=["Lf"])
            P.emit()

    def rope(self, ps, pskey, tc, tckey, ts, tskey, r1, r1key, r2, r2key, outb, outkey, pt):
        P = self.P
        P.op("dve", lambda e: e.tensor_tensor(out=r1[0:pt], in0=ps[0:pt], in1=tc[0:pt], op=ALU.mult),
             reads=[pskey, tckey], writes=[r1key])
        P.op("dve", lambda e: e.tensor_tensor(out=r2[0:pt, :, 0:64], in0=ps[0:pt, :, 64:128], in1=ts[0:pt, :, 0:64],
                                              op=ALU.mult), reads=[pskey, tskey], writes=[r2key + "a"])
        P.op("dve", lambda e: e.tensor_tensor(out=r2[0:pt, :, 64:128], in0=ps[0:pt, :, 0:64], in1=ts[0:pt, :, 64:128],
                                              op=ALU.mult), reads=[pskey, tskey], writes=[r2key + "b"])
        P.op("pool", lambda e: e.tensor_tensor(out=outb[0:pt], in0=r1[0:pt], in1=r2[0:pt], op=ALU.add),
             reads=[r1key, r2key + "a", r2key + "b"], writes=[outkey])

    def setup_mixer_consts(self):
        P, st = self.P, self.top
        gam = [1.0 - 2.0 ** (-5.0 - h) for h in range(NH)]
        self.gc128 = self.sb(st, "gc128", [128, NH, HD], F32)
        self.gc32 = self.sb(st, "gc32", [128, NH, HD], F32)
        for h in range(NH):
            P.op("dve", lambda e, h=h: e.memset(self.gc128[:, h, :], float(np.float32(gam[h]) ** 128)), writes=["gc128"])
            P.op("dve", lambda e, h=h: e.memset(self.gc32[:, h, :], float(np.float32(gam[h]) ** 32)), writes=["gc32"])
        self.mask = self.sb(st, "mask_sb", [128, NH, HD], F32)
        md = self.dr["maskT"]
        P.dma("sp", lambda e, s: e.dma_start(out=self.mask[:], in_=md.ap().rearrange("p (h d) -> p h d", h=NH)).then_inc(s, 16),
              "c_mask", writes=["mask"])
        self.ones128 = self.sb(st, "ones128", [128, 128], BF16)
        self.ones512 = self.sb(st, "ones512", [128, 128], BF16)
        P.op("dve", lambda e: e.memset(self.ones128[:], 1.0 / 128.0), writes=["ones128"])
        P.op("dve", lambda e: e.memset(self.ones512[:], 1.0 / 512.0), writes=["ones512"])

    def colstats(self, tag, chunks, n, ones, bk_mean, kmean, bk_msq, kmsq, tmp):
        P = self.P
        sqb, cpb, m2, mean_sb, rstd_sb = tmp
        nchunk = len(chunks)
        for ci, (ap, key) in enumerate(chunks):
            P.op("act", lambda e, ap=ap: e.activation(out=sqb[:, 0:n], in_=ap, func=AF.Square),
                 reads=[key], writes=[tag + "sqb"])
            P.op("act", lambda e, ap=ap: e.copy(out=cpb[:, 0:n], in_=ap), reads=[key], writes=[tag + "cpb"])
            P.op("pe", lambda e, ci=ci: e.matmul(bk_mean[:, 0:n], lhsT=ones[:, :], rhs=cpb[:, 0:n],
                                                 start=(ci == 0), stop=(ci == nchunk - 1)),
                 reads=[tag + "cpb"], writes=[kmean])
            P.op("pe", lambda e, ci=ci: e.matmul(bk_msq[:, 0:n], lhsT=ones[:, :], rhs=sqb[:, 0:n],
                                                 start=(ci == 0), stop=(ci == nchunk - 1)),
                 reads=[tag + "sqb"], writes=[kmsq])
        P.op("act", lambda e: e.activation(out=m2[:, 0:n], in_=bk_mean[:, 0:n], func=AF.Square),
             reads=[kmean], writes=[tag + "m2"])
        P.op("act", lambda e: e.copy(out=mean_sb[:, 0:n], in_=bk_mean[:, 0:n]), reads=[kmean], writes=[tag + "mean"])
        P.op("dve", lambda e: e.tensor_tensor(out=rstd_sb[:, 0:n], in0=bk_msq[:, 0:n], in1=m2[:, 0:n], op=ALU.subtract),
             reads=[kmsq, tag + "m2"], writes=[tag + "var"])
        P.op("dve", lambda e: e.tensor_scalar(out=rstd_sb[:, 0:n], in0=rstd_sb[:, 0:n], scalar1=0.0, scalar2=None,
                                              op0=ALU.max), reads=[tag + "var"], writes=[tag + "var2"])
        P.op("act", lambda e: e.activation(out=rstd_sb[:, 0:n], in_=rstd_sb[:, 0:n], func=AF.Sqrt,
                                           bias=self.eps_t[:, :]), reads=[tag + "var2", "eps_t"], writes=[tag + "std"])
        P.op("dve", lambda e: e.reciprocal(out=rstd_sb[:, 0:n], in_=rstd_sb[:, 0:n]),
             reads=[tag + "std"], writes=[tag + "rstd"])
        return tag + "mean", tag + "rstd"

    def load_cols(self, stack, name, dram_vec, nf):
        t = self.sb(stack, name, [128, nf], F32)
        src = dram_vec.ap().rearrange("o (f p) -> p (o f)", p=128)
        self.P.dma("sp", lambda e, s: e.dma_start(out=t[:], in_=src, allow_slow_non_contiguous=True).then_inc(s, 16),
                   "c_" + name, writes=[name])
        return t

    def mixer_phase(self, tiles, x1s, x2s, ks, vs, w_in, w_out, gmix, gpost, tqc, tqs, conv_w, conv_b, lng, lnb,
                    gng, state_conv, state_ret, Lall, coef, o_nconv_p, o_nret_p, o_nconv_s, o_nret_s):
        P, nc = self.P, self.nc
        with ExitStack() as st:
            NTm = 512
            g_b = self.load_bcast(st, "mxg", gmix, D)
            gp_b = self.load_bcast(st, "mxgp", gpost, D)
            w_v = w_in.ap().rearrange("(kc p) c -> p kc c", p=128)
            wa = self.sb(st, "mx_wa", [128, KC, 512], BF16)
            wb = self.sb(st, "mx_wb", [128, KC, 512], BF16)
            wq = self.sb(st, "mx_wq", [128, KC, 512], BF16)
            wgt = self.sb(st, "mx_wg", [128, KC, 512], BF16)
            wo = self.sb(st, "mx_wo", [128, KC, D], BF16)
            for nm, t, c0 in (("wa", wa, 0), ("wb", wb, 512), ("wq", wq, 1024), ("wgt", wgt, 2560)):
                P.dma("pool", lambda e, s, t=t, c0=c0: e.dma_start(out=t[:], in_=w_v[:, :, c0:c0 + 512]).then_inc(s, 16),
                      "mx_" + nm, writes=[nm])
            wo_v = w_out.ap().rearrange("(kc p) c -> p kc c", p=128)
            P.dma("pool", lambda e, s: e.dma_start(out=wo[:], in_=wo_v).then_inc(s, 16), "mx_wo", writes=["wo"])
            cb_c = self.load_cols(st, "mx_cb", conv_b, 4)
            lng_c = self.load_cols(st, "mx_lng", lng, 4)
            lnb_c = self.load_cols(st, "mx_lnb", lnb, 4)
            gng_c = self.load_cols(st, "mx_gng", gng, 4)

            xin = [self.sb(st, f"mx_x{i}", [128, D], F32) for i in range(2)]
            tcq = [self.sb(st, f"mx_tc{i}", [128, NH, HD], F32) for i in range(2)]
            tsq = [self.sb(st, f"mx_ts{i}", [128, NH, HD], F32) for i in range(2)]
            junk = self.sb(st, "mx_junk", [128, D], BF16)
            ssc = [self.sb(st, f"mx_ss{i}", [128, 1], F32) for i in range(2)]
            rsc = [self.sb(st, f"mx_rs{i}", [128, 1], F32) for i in range(2)]
            htok = [self.sb(st, f"mx_h{i}", [128, D], BF16) for i in range(2)]
            hT = self.sb(st, "mx_hT", [128, KC, NTm], BF16)
            r1 = self.sb(st, "mx_r1", [128, NH, HD], F32)
            r2 = self.sb(st, "mx_r2", [128, NH, HD], F32)
            qb = [self.sb(st, f"mx_qb{i}", [128, NH, HD], BF16) for i in range(2)]
            ktk = [self.sb(st, f"mx_kt{i}", [128, NH, HD], BF16) for i in range(4)]
            vtk = [self.sb(st, f"mx_vt{i}", [128, NH, HD], BF16) for i in range(4)]
            qT = self.sb(st, "mx_qT", [128, NH, NTm], BF16)
            kT = self.sb(st, "mx_kT", [128, NH, NTm], BF16)
            uext = self.sb(st, "mx_uext", [128, 4, 30 + NTm], F32)
            uexs = self.sb(st, "mx_uexs", [128, 4, 30 + 64], F32)
            tb = [self.sb(st, f"mx_tb{i}", [128, NTm], F32) for i in range(2)]
            gs = self.sb(st, "mx_gs", [128, 4, NTm], F32)
            acc = self.sb(st, "mx_acc", [128, 4, NTm], F32)
            catT = self.sb(st, "mx_catT", [128, 8, NTm], BF16)
            sqb = self.sb(st, "mx_sqb", [128, NTm], BF16)
            cpb = self.sb(st, "mx_cpb", [128, NTm], BF16)
            m2 = self.sb(st, "mx_m2", [128, NTm], F32)
            mean_sb = self.sb(st, "mx_mean", [128, NTm], F32)
            rstd_sb = self.sb(st, "mx_rstd", [128, NTm], F32)
            cen = self.sb(st, "mx_cen", [128, NTm], F32)
            ub = self.sb(st, "mx_ub", [128, 4, 30 + NTm], BF16)
            dgf = [self.sb(st, f"mx_dgf{i}", [128, CW, 128], BF16) for i in range(2)]
            of32 = self.sb(st, "mx_of", [128, NH, HD], F32)
            sTm = self.sb(st, "mx_sTm", [128, NH, HD], BF16)
            Rf = self.sb(st, "mx_Rf", [128, NH, HD], F32)
            Rb = self.sb(st, "mx_Rb", [128, NH, HD], BF16)
            dsb = self.sb(st, "mx_dsb", [128, D], F32)
            xo = [self.sb(st, f"mx_xo{i}", [128, D], F32) for i in range(2)]
            cwt = self.sb(st, "mx_misc", [32, 512], F32)
            cwh = self.sb(st, "mx_cwh", [128, 4, 32], F32)
            hst = cwt
            oco = cwt
            psT = self.ps(st, "mx_psT", [128, KC, 128], BF16)
            bk = [self.ps(st, f"mx_bk{i}", [128, 512], F32) for i in range(7)]

            def bk3(i):
                return bk[i][:].rearrange("p (h d) -> p h d", h=NH)

            P.dma("sp", lambda e, s: e.dma_start(out=cwt[0:CW, :], in_=conv_w[0:CW, :]).then_inc(s, 16),
                  "mx_cwt", writes=["cwt"])

            def tr_f32(src_t, rows, dst_ap_fn, scale, skey, dkey):
                for fb in range(4):
                    P.op("pe", lambda e, fb=fb: e.transpose(out=bk[1][:, 0:rows],
                                                            in_=src_t[0:rows, fb * 128:(fb + 1) * 128],
                                                            identity=self.ident_f[0:rows, 0:rows]),
                         reads=[skey, "ident_f"], writes=[("bk", 1)])
                    P.op("act", lambda e, fb=fb: e.activation(out=dst_ap_fn(fb), in_=bk[1][:, 0:rows], func=AF.Copy,
                                                              scale=scale),
                         reads=[("bk", 1)], writes=[dkey])
            tr_f32(cwt, CW, lambda fb: cwh[:, fb, 0:CW], 0.5, "cwt", "cwh")

            RfP = self.RfP
            pacc = [self.sb(st, f"mx_pacc{i}", [128, 1], F32) for i in range(3)]
            cnt = {"x": 0, "kv": 0, "xo": 0, "tb": 0, "pr": 0, "dg": 0}
            rot = [2, 3, 4, 5]

            def next_bank():
                b = rot[cnt["pr"] % 4]
                cnt["pr"] += 1
                return b

            def tile_body(ti, tile):
                blks = tile["blocks"]
                isS = tile["kind"] == "S"
                NT = sum(b[1] for b in blks)
                ue = uexs if isS else uext
                if isS:
                    P.dma("sp", lambda e, s: e.dma_start(
                        out=Rf[:], in_=state_ret.ap().rearrange("h d v -> d h v")).then_inc(s, 16),
                        "mx_R0", writes=["Rf"])
                    P.op("act", lambda e: e.copy(out=Rb[:], in_=Rf[:]), reads=["Rf"], writes=["Rb"])
                    P.dma("sp", lambda e, s: e.dma_start(out=hst[0:30, :], in_=state_conv[0:30, :]).then_inc(s, 16),
                          "mx_hst", writes=["hst", "cwt", "oco"])
                    tr_f32(hst, 30, lambda fb: uexs[:, fb, 0:30], 2.0, "hst", "uexs_h")
                elif tiles[ti - 1]["kind"] == "S":
                    if DEBUG_OUT:
                        dbgh = self.dr["dbg_h"]
                        P.dma("sp", lambda e, s: e.dma_start(out=dbgh[:, :], in_=uexs[:].rearrange("p f c -> p (f c)")).then_inc(s, 16),
                              "dbg_h", reads=["uexs_h"] + [("uu", f) for f in range(4)] + [("acc", f) for f in range(4)])
                    P.op("dve", lambda e: e.tensor_copy(out=Rf[:], in_=RfP[:]), reads=["RfP", "Rf"], writes=["Rf"])
                    P.op("act", lambda e: e.copy(out=Rb[:], in_=RfP[:]), reads=["RfP", "Rb"], writes=["Rb"])
                    P.op("pool", lambda e: e.tensor_copy(out=uext[:, :, 0:30], in_=uexs[:, :, 64:94]),
                         reads=["uexs_h"] + [("uu", f) for f in range(4)], writes=["uext_h"])
                else:
                    P.op("pool", lambda e: e.tensor_copy(out=uext[:, :, 0:30], in_=uext[:, :, NTm:NTm + 30]),
                         reads=[("uu", f) for f in range(4)], writes=["uext_h"])
                c0 = 0
                blkinfo = []
                for b, (r0, pt) in enumerate(blks):
                    s = cnt["x"] % 2
                    cnt["x"] += 1
                    kvs = cnt["kv"] % 4
                    cnt["kv"] += 1
                    P.dma("sp", lambda e, sm, s=s, r0=r0, pt=pt: e.dma_start(
                        out=xin[s][0:pt, :], in_=x1s[r0:r0 + pt, :]).then_inc(sm, 16), f"mx_x{s}", writes=[f"x{s}"])
                    P.dma("sp", lambda e, sm, s=s, r0=r0, pt=pt: e.dma_start(
                        out=tcq[s][0:pt], in_=tqc[r0:r0 + pt, :].rearrange("p (h d) -> p h d", h=NH)).then_inc(sm, 16),
                        f"mx_tc{s}", writes=[f"tc{s}"])
                    P.dma("sp", lambda e, sm, s=s, r0=r0, pt=pt: e.dma_start(
                        out=tsq[s][0:pt], in_=tqs[r0:r0 + pt, :].rearrange("p (h d) -> p h d", h=NH)).then_inc(sm, 16),
                        f"mx_ts{s}", writes=[f"ts{s}"])
                    P.dma("sp", lambda e, sm, kvs=kvs, r0=r0, pt=pt: e.dma_start(
                        out=ktk[kvs][0:pt], in_=ks[r0:r0 + pt, :].rearrange("p (h d) -> p h d", h=NH)).then_inc(sm, 16),
                        f"mx_kt{kvs}", writes=[f"kt{kvs}"])
                    P.dma("sp", lambda e, sm, kvs=kvs, r0=r0, pt=pt: e.dma_start(
                        out=vtk[kvs][0:pt], in_=vs[r0:r0 + pt, :].rearrange("p (h d) -> p h d", h=NH)).then_inc(sm, 16),
                        f"mx_vt{kvs}", writes=[f"vt{kvs}"])
                    self.norm_block("mx", xin[s], f"x{s}", pt, g_b, "mxg", htok[s], f"h{s}", ssc[s], rsc[s],
                                    f"ss{s}", junk)
                    self.transpose_to(htok[s], f"h{s}", pt, KC, psT, hT[:, :, c0:c0 + pt], [("hT", b)])

                    def projq(e, c0=c0, pt=pt):
                        ins = None
                        for kc in range(KC):
                            ins = e.matmul(bk[0][0:pt, :], lhsT=hT[:, kc, c0:c0 + pt], rhs=wq[:, kc, :],
                                           start=(kc == 0), stop=(kc == KC - 1))
                        return ins
                    P.op("pe", projq, reads=["wq", ("hT", b)], writes=[("bk", 0)])
                    self.rope(bk3(0), ("bk", 0), tcq[s], f"tc{s}", tsq[s], f"ts{s}", r1, "r1", r2, "r2",
                              qb[s], f"qb{s}", pt)
                    self.transpose_to(qb[s].ap().rearrange("p h d -> p (h d)"), f"qb{s}", pt, NH, psT,
                                      qT[:, :, c0:c0 + pt], [("qT", b)])
                    self.transpose_to(ktk[kvs].ap().rearrange("p h d -> p (h d)"), f"kt{kvs}", pt, NH, psT,
                                      kT[:, :, c0:c0 + pt], [("kT", b)], eng="dve")
                    blkinfo.append((b, r0, pt, c0, kvs))
                    c0 += pt
                nb = len(blks)
                hT_keys = [("hT", b) for b in range(nb)]

                def proj_fm(w, fb, bank):
                    def f(e):
                        ins = None
                        for kc in range(KC):
                            ins = e.matmul(bk[bank][:, 0:NT], lhsT=w[:, kc, fb * 128:(fb + 1) * 128],
                                           rhs=hT[:, kc, 0:NT], start=(kc == 0), stop=(kc == KC - 1))
                        return ins
                    return f
                for fb in range(4):
                    ba, bb = next_bank(), next_bank()
                    P.op("pe", proj_fm(wa, fb, ba), reads=["wa"] + hT_keys, writes=[("bk", ba)])
                    P.op("pe", proj_fm(wb, fb, bb), reads=["wb"] + hT_keys, writes=[("bk", bb)])
                    tbi = cnt["tb"] % 2
                    cnt["tb"] += 1
                    P.op("act", lambda e, bb=bb, tbi=tbi: e.activation(out=tb[tbi][:, 0:NT], in_=bk[bb][:, 0:NT],
                                                                      func=AF.Tanh, scale=0.5),
                         reads=[("bk", bb)], writes=[("tb", tbi)])
                    P.op("dve", lambda e, ba=ba, tbi=tbi, fb=fb: e.scalar_tensor_tensor(
                        out=ue[:, fb, 30:30 + NT], in0=tb[tbi][:, 0:NT], scalar=1.0, in1=bk[ba][:, 0:NT],
                        op0=ALU.add, op1=ALU.mult),
                        reads=[("tb", tbi), ("bk", ba)], writes=[("uu", fb)])
                for fb in range(4):
                    bg = next_bank()
                    P.op("pe", proj_fm(wgt, fb, bg), reads=["wgt"] + hT_keys, writes=[("bk", bg)])
                    P.op("act", lambda e, bg=bg, fb=fb: e.activation(out=gs[:, fb, 0:NT], in_=bk[bg][:, 0:NT], func=AF.Silu),
                         reads=[("bk", bg)], writes=[("gs", fb)])
                hkey = "uexs_h" if isS else "uext_h"
                for fb in range(4):
                    ds_ = cnt["dg"] % 2
                    cnt["dg"] += 1
                    for w in range(CW):
                        P.op("dve", lambda e, fb=fb, w=w, ds_=ds_: e.tensor_scalar(
                            out=dgf[ds_][:, w, :], in0=self.ident_f[:, :], scalar1=cwh[:, fb, w:w + 1], scalar2=None,
                            op0=ALU.mult), reads=["cwh", "ident_f"], writes=[("dgf", ds_, w)])
                    P.op("act", lambda e, fb=fb: e.copy(out=ub[:, fb, 0:30 + NT], in_=ue[:, fb, 0:30 + NT]),
                         reads=[("uu", fb), hkey], writes=[("ub", fb)])
                    bc = next_bank()

                    def cv(e, fb=fb, ds_=ds_, bc=bc):
                        ins = None
                        for w in range(CW):
                            ins = e.matmul(bk[bc][:, 0:NT], lhsT=dgf[ds_][:, w, :], rhs=ub[:, fb, w:w + NT],
                                           start=(w == 0), stop=(w == CW - 1))
                        return ins
                    P.op("pe", cv, reads=[("ub", fb)] + [("dgf", ds_, w) for w in range(CW)], writes=[("bk", bc)])
                    P.op("act", lambda e, fb=fb, bc=bc: e.activation(out=acc[:, fb, 0:NT], in_=bk[bc][:, 0:NT],
                                                                      func=AF.Identity, bias=cb_c[:, fb:fb + 1]),
                         reads=[("bk", bc), "mx_cb"], writes=[("acc", fb)])
                mk, rk = self.colstats("ln", [(acc[:, fb, 0:NT], ("acc", fb)) for fb in range(4)], NT, self.ones512,
                                       bk[1], ("bk", 1), bk[0], ("bk", 0), (sqb, cpb, m2, mean_sb, rstd_sb))
                for fb in range(4):
                    P.op("dve", lambda e, fb=fb: e.tensor_tensor(out=cen[:, 0:NT], in0=acc[:, fb, 0:NT],
                                                                 in1=mean_sb[:, 0:NT], op=ALU.subtract),
                         reads=[("acc", fb), mk], writes=["cen"])
                    P.op("pool", lambda e: e.tensor_tensor(out=cen[:, 0:NT], in0=cen[:, 0:NT], in1=rstd_sb[:, 0:NT],
                                                           op=ALU.mult), reads=["cen", rk], writes=["cen"])
                    P.op("act", lambda e, fb=fb: e.activation(out=catT[:, fb, 0:NT], in_=cen[:, 0:NT], func=AF.Silu,
                                                              scale=lng_c[:, fb:fb + 1], bias=lnb_c[:, fb:fb + 1]),
                         reads=["cen", "mx_lng", "mx_lnb"], writes=[("cat", fb)])
                for (b, r0, pt, c0, kvs) in blkinfo:
                    nt = tile["nt_ret"]
                    gcs = self.gc32 if isS else self.gc128
                    gkey = "gc32" if isS else "gc128"

                    def sc(e, c0=c0, nt=nt):
                        ins = None
                        for h in range(NH):
                            ins = e.matmul(bk3(4)[0:nt, h, 0:nt], lhsT=kT[:, h, c0:c0 + nt], rhs=qT[:, h, c0:c0 + nt],
                                           start=True, stop=True)
                        return ins
                    P.op("pe", sc, reads=[("kT", b), ("qT", b)], writes=[("bk", 4)])
                    P.op("dve", lambda e, nt=nt: e.tensor_tensor(out=sTm[0:nt, :, 0:nt], in0=bk3(4)[0:nt, :, 0:nt],
                                                                 in1=self.mask[0:nt, :, 0:nt], op=ALU.mult),
                         reads=[("bk", 4), "mask"], writes=["sTm"])

                    def ot(e, c0=c0, nt=nt, kvs=kvs):
                        ins = None
                        for h in range(NH):
                            e.matmul(bk3(5)[:, h, 0:nt], lhsT=vtk[kvs][0:nt, h, :], rhs=sTm[0:nt, h, 0:nt],
                                     start=True, stop=False)
                            ins = e.matmul(bk3(5)[:, h, 0:nt], lhsT=Rb[:, h, :], rhs=qT[:, h, c0:c0 + nt],
                                           start=False, stop=True)
                        return ins
                    P.op("pe", ot, reads=[f"vt{kvs}", "sTm", "Rb", ("qT", b)], writes=[("bk", 5)])

                    def mu(e, nt=nt, kvs=kvs):
                        ins = None
                        for h in range(NH):
                            ins = e.matmul(bk3(6)[:, h, :], lhsT=ktk[kvs][0:nt, h, :], rhs=vtk[kvs][0:nt, h, :],
                                           start=True, stop=True)
                        return ins
                    P.op("pe", mu, reads=[f"kt{kvs}", f"vt{kvs}"], writes=[("bk", 6)])
                    P.op("dve", lambda e: e.tensor_tensor(out=Rf[:], in0=Rf[:], in1=bk3(6), op=ALU.add),
                         reads=["Rf", ("bk", 6)], writes=["Rf"])
                    P.op("pool", lambda e, gcs=gcs: e.tensor_tensor(out=Rf[:], in0=Rf[:], in1=gcs[:], op=ALU.mult),
                         reads=["Rf", gkey], writes=["Rf"])
                    P.op("pool", lambda e: e.tensor_copy(out=Rb[:], in_=Rf[:]), reads=["Rf"], writes=["Rb"])
                    P.op("dve", lambda e, nt=nt: e.tensor_copy(out=of32[:, :, 0:nt], in_=bk3(5)[:, :, 0:nt]),
                         reads=[("bk", 5)], writes=["of32"])
                    n4 = NH * nt
                    ofv = of32[:, :, 0:nt]
                    P.op("act", lambda e, nt=nt, n4=n4: e.activation(
                        out=sqb[:, 0:n4].rearrange("p (h t) -> p h t", h=NH), in_=of32[:, :, 0:nt], func=AF.Square),
                        reads=["of32"], writes=["gnsqb"])
                    P.op("act", lambda e, nt=nt, n4=n4: e.copy(
                        out=cpb[:, 0:n4].rearrange("p (h t) -> p h t", h=NH), in_=of32[:, :, 0:nt]),
                        reads=["of32"], writes=["gncpb"])
                    P.op("pe", lambda e, n4=n4: e.matmul(bk[1][:, 0:n4], lhsT=self.ones128[:, :], rhs=cpb[:, 0:n4],
                                                         start=True, stop=True), reads=["gncpb"], writes=[("bk", 1)])
                    P.op("pe", lambda e, n4=n4: e.matmul(bk[0][:, 0:n4], lhsT=self.ones128[:, :], rhs=sqb[:, 0:n4],
                                                         start=True, stop=True), reads=["gnsqb"], writes=[("bk", 0)])
                    P.op("act", lambda e, n4=n4: e.activation(out=m2[:, 0:n4], in_=bk[1][:, 0:n4], func=AF.Square),
                         reads=[("bk", 1)], writes=["gnm2"])
                    P.op("dve", lambda e, n4=n4: e.tensor_tensor(out=rstd_sb[:, 0:n4], in0=bk[0][:, 0:n4], in1=m2[:, 0:n4],
                                                                 op=ALU.subtract), reads=[("bk", 0), "gnm2"], writes=["gnvar"])
                    P.op("dve", lambda e, n4=n4: e.tensor_scalar(out=rstd_sb[:, 0:n4], in0=rstd_sb[:, 0:n4], scalar1=0.0,
                                                                 scalar2=None, op0=ALU.max), reads=["gnvar"], writes=["gnvar2"])
                    P.op("act", lambda e, n4=n4: e.activation(out=rstd_sb[:, 0:n4], in_=rstd_sb[:, 0:n4], func=AF.Sqrt,
                                                              bias=self.eps_t[:, :]), reads=["gnvar2", "eps_t"], writes=["gnstd"])
                    P.op("dve", lambda e, n4=n4: e.reciprocal(out=rstd_sb[:, 0:n4], in_=rstd_sb[:, 0:n4]),
                         reads=["gnstd"], writes=["gnrstd"])
                    P.op("dve", lambda e, nt=nt, n4=n4: e.tensor_tensor(
                        out=cen[:, 0:n4].rearrange("p (h t) -> p h t", h=NH), in0=of32[:, :, 0:nt],
                        in1=bk[1][:, 0:n4].rearrange("p (h t) -> p h t", h=NH), op=ALU.subtract),
                        reads=["of32", ("bk", 1)], writes=["gncen"])
                    P.op("pool", lambda e, n4=n4: e.tensor_tensor(out=cen[:, 0:n4], in0=cen[:, 0:n4], in1=rstd_sb[:, 0:n4],
                                                                  op=ALU.mult), reads=["gncen", "gnrstd"], writes=["gncen"])
                    for h in range(NH):
                        P.op("dve", lambda e, h=h, nt=nt, c0=c0: e.scalar_tensor_tensor(
                            out=catT[:, 4 + h, c0:c0 + nt], in0=cen[:, h * nt:(h + 1) * nt], scalar=gng_c[:, h:h + 1],
                            in1=gs[:, h, c0:c0 + nt], op0=ALU.mult, op1=ALU.mult),
                            reads=["gncen", "mx_gng", ("gs", h)], writes=[("cat", 4 + h, b)])
                    if nt < pt:
                        P.op("pool", lambda e, nt=nt, pt=pt, c0=c0: e.memset(catT[:, 4:8, c0 + nt:c0 + pt], 0.0),
                             writes=[("catpad", b)])
                for (b, r0, pt, c0, kvs) in blkinfo:
                    s2 = cnt["x"] % 2
                    cnt["x"] += 1
                    P.dma("sp", lambda e, sm, s2=s2, r0=r0, pt=pt: e.dma_start(
                        out=xin[s2][0:pt, :], in_=x1s[r0:r0 + pt, :]).then_inc(sm, 16), f"mx_x{s2}", writes=[f"x{s2}"])
                    catkeys = [("cat", f) for f in range(4)] + [("cat", 4 + h, b) for h in range(NH)] + [("catpad", b)]
                    for hh in range(2):
                        def mo(e, hh=hh, c0=c0, pt=pt):
                            ins = None
                            for kc in range(8):
                                ins = e.matmul(bk[2 + hh][0:pt, :], lhsT=catT[:, kc, c0:c0 + pt],
                                               rhs=wo[:, kc, hh * 512:(hh + 1) * 512], start=(kc == 0), stop=(kc == 7))
                            return ins
                        P.op("pe", mo, reads=catkeys + ["wo"], writes=[("bk", 2 + hh)])
                        P.op("act", lambda e, hh=hh, pt=pt, s2=s2: e.activation(
                            out=junk[0:pt, 0:512], in_=bk[2 + hh][0:pt, :], func=AF.Square,
                            accum_out=pacc[hh][0:pt, :]),
                            reads=[("bk", 2 + hh)], writes=[f"nss{hh}"])
                        P.op("dve", lambda e, hh=hh, pt=pt: e.tensor_copy(out=dsb[0:pt, hh * 512:(hh + 1) * 512],
                                                                          in_=bk[2 + hh][0:pt, :]),
                             reads=[("bk", 2 + hh), f"nss{hh}"], writes=[("dsb", hh)])
                    P.op("dve", lambda e, pt=pt: e.tensor_tensor(out=pacc[2][0:pt, :], in0=pacc[0][0:pt, :],
                                                                 in1=pacc[1][0:pt, :], op=ALU.add),
                         reads=["nss0", "nss1"], writes=["nss"])
                    P.op("act", lambda e, pt=pt: e.activation(out=pacc[2][0:pt, :], in_=pacc[2][0:pt, :], func=AF.Sqrt,
                                                              scale=1.0 / D, bias=self.eps_t[0:pt, :]),
                         reads=["nss", "eps_t"], writes=["nssq"])
                    P.op("dve", lambda e, pt=pt: e.reciprocal(out=pacc[2][0:pt, :], in_=pacc[2][0:pt, :]),
                         reads=["nssq"], writes=["nssr"])
                    xs = cnt["xo"] % 2
                    cnt["xo"] += 1
                    P.op("dve", lambda e, pt=pt, s2=s2: e.scalar_tensor_tensor(
                        out=dsb[0:pt, :], in0=dsb[0:pt, :], scalar=pacc[2][0:pt, :], in1=gp_b[0:pt, :],
                        op0=ALU.mult, op1=ALU.mult), reads=[("dsb", 0), ("dsb", 1), "nssr", "mxgp"],
                        writes=[("dsb", 0), ("dsb", 1)])
                    P.op("pool", lambda e, pt=pt, s2=s2, xs=xs: e.tensor_tensor(out=xo[xs][0:pt, :], in0=dsb[0:pt, :],
                                                                               in1=xin[s2][0:pt, :], op=ALU.add),
                         reads=[("dsb", 0), ("dsb", 1), f"x{s2}"], writes=[f"xo{xs}"])
                    P.dma("sp", lambda e, sm, xs=xs, r0=r0, pt=pt: e.dma_start(
                        out=x2s[r0:r0 + pt, :], in_=xo[xs][0:pt, :]).then_inc(sm, 16), f"mx_xo{xs}", reads=[f"xo{xs}"])
                last_prompt = (not isS) and ti == len(tiles) - 1
                if isS or last_prompt:
                    o_c = o_nconv_s if isS else o_nconv_p
                    o_r = o_nret_s if isS else o_nret_p
                    col0 = 32 if isS else NTm
                    for fb in range(4):
                        P.op("pe", lambda e, fb=fb, col0=col0: e.transpose(
                            out=bk[1][0:30, fb * 128:(fb + 1) * 128], in_=ue[:, fb, col0:col0 + 30],
                            identity=self.ident_f[:, :]),
                            reads=[("uu", fb), hkey, "ident_f"], writes=[("bk", 1)])
                    P.op("act", lambda e: e.activation(out=oco[0:30, :], in_=bk[1][0:30, :], func=AF.Copy, scale=0.5),
                         reads=[("bk", 1)], writes=["oco", "hst", "cwt"])
                    P.dma("sp", lambda e, sm, o_c=o_c: e.dma_start(out=o_c[0:30, :], in_=oco[0:30, :]).then_inc(sm, 16),
                          "mx_oco", reads=["oco"])
                    P.dma("sp", lambda e, sm, o_r=o_r: e.dma_start(
                        out=o_r.ap().rearrange("h d v -> d h v"), in_=Rf[:]).then_inc(sm, 16), "mx_oR", reads=["Rf"])

            for ti, tile in enumerate(tiles):
                tile_body(ti, tile)
            P.emit()

    def exchange_phase(self, Lsend, Lall, coef, use_cc=True):
        P = self.P
        self.RfP = self.sb(self.top, "RfP", [128, NH, HD], F32)
        RfP = self.RfP
        with ExitStack() as st:
            Lg = self.sb(st, "ex_Lg", [128, NCORES, 512], F32)
            coef_b = self.load_bcast(st, "ex_coef", coef, 32)
            if use_cc and not SKIP_CC:
                P.dma("pool", lambda e, s: e.collective_compute(
                    "AllGather", ALU.bypass, replica_groups=[list(range(NCORES))],
                    ins=[Lsend.ap().opt()], outs=[Lall.ap().opt()]).then_inc(s),
                    "cc", inc=1, reads=["Lsend"], writes=["Lall"])
            elif use_cc:
                for r in range(NCORES):
                    P.dma("sp", lambda e, s, r=r: e.dma_start(out=Lall[r * 128:(r + 1) * 128, :], in_=Lsend[:, :]).then_inc(s, 16),
                          "cc", reads=["Lsend"], writes=["Lall"])
            P.dma("sp", lambda e, s: e.dma_start(out=Lg[:], in_=Lall.ap().rearrange("(r p) c -> p r c", p=128)).then_inc(s, 16),
                  "ex_Lg", reads=["Lall"], writes=["Lg"])
            P.op("dve", lambda e: e.memset(RfP[:], 0.0), writes=["RfP"])
            for r in range(NCORES):
                for h in range(NH):
                    P.op("dve", lambda e, r=r, h=h: e.scalar_tensor_tensor(
                        out=RfP[:, h, :], in0=Lg[:, r, h * 128:(h + 1) * 128],
                        scalar=coef_b[:, r * 4 + h:r * 4 + h + 1], in1=RfP[:, h, :], op0=ALU.mult, op1=ALU.add),
                        reads=["Lg", "ex_coef", "RfP"], writes=["RfP"])
            P.emit()


def _tables():
    half = HD // 2
    inv = (np.float32(10000.0) ** (-np.arange(half, dtype=np.float32) / np.float32(half))).astype(np.float32)
    gam = np.array([1.0 - 2.0 ** (-5.0 - h) for h in range(NH)], dtype=np.float64)
    out = []
    for c in range(NCORES):
        seg = c % 4
        pos = np.concatenate([seg * SEG + np.arange(SEG), PAST + np.arange(32), np.zeros(32)]).astype(np.float32)
        idx = np.concatenate([np.arange(SEG) % 128, np.arange(32), np.zeros(32)]).astype(np.float64)
        ang = (pos[:, None] * inv[None, :]).astype(np.float32)
        cos = np.cos(ang).astype(np.float32)
        sin = np.sin(ang).astype(np.float32)
        cosF = np.concatenate([cos, cos], axis=1).astype(np.float64)
        sinS = np.concatenate([-sin, sin], axis=1).astype(np.float64)
        xi = gam[None, :] ** (idx[:, None] + 1.0)
        kd = gam[None, :] ** (-(idx[:, None] + 1.0)) * (HD ** -0.5)
        tqc = (cosF[:, None, :] * xi[:, :, None]).reshape(NTOK, NH * HD).astype(np.float32)
        tqs = (sinS[:, None, :] * xi[:, :, None]).reshape(NTOK, NH * HD).astype(np.float32)
        tkc = (cosF[:, None, :] * kd[:, :, None]).reshape(NTOK, NH * HD).astype(np.float32)
        tks = (sinS[:, None, :] * kd[:, :, None]).reshape(NTOK, NH * HD).astype(np.float32)
        coef = np.zeros((1, NCORES * NH), np.float32)
        for r in range(NCORES):
            if r // 4 == c // 4 and r < c:
                coef[0, r * NH:(r + 1) * NH] = (gam ** (SEG * (c - 1 - r))).astype(np.float32)
        out.append(dict(tqc=tqc, tqs=tqs, tkc=tkc, tks=tks, coef=coef))
    return out


_CACHE = {}
TWO_LAUNCH = True


def build_program(mode="fused"):
    if ("nc", mode) in _CACHE:
        return _CACHE[("nc", mode)]
    B = Builder({})
    nc = B.nc
    B.din("ident", [128, 128])
    B.din("maskT", [128, NH * HD])
    doA = mode in ("fused", "A")
    doB = mode in ("fused", "B")
    names2 = {}
    for f, need in (("ffn1", doA), ("ffn2", doB)):
        if need:
            names2[f] = dict(pre=B.din(f + "_norm_pre", [1, D]), wg=B.din(f + "_w_gate", [D, DFF]),
                             wu=B.din(f + "_w_up", [D, DFF]), wd=B.din(f + "_w_down", [DFF, D]),
                             post=B.din(f + "_norm_post", [1, D]))
    gmix = B.din("mix_norm_pre", [1, D])
    w_in = B.din("w_in", [D, 3072])
    if doA:
        x = B.din("x_loc", [NTOK, D])
        tkc = B.din("tkc", [NTOK, 512]); tks = B.din("tks", [NTOK, 512])
    if doB:
        conv_w = B.din("conv_w", [CW, DCONV])
        conv_b = B.din("conv_b", [1, DCONV])
        lng = B.din("conv_ln_g", [1, DCONV])
        lnb = B.din("conv_ln_b", [1, DCONV])
        gng = B.din("ret_gn_g", [1, DRET])
        w_out = B.din("w_out", [D, D])
        gmixp = B.din("mix_norm_post", [1, D])
        st_conv = B.din("state_conv_c", [30, DCONV])
        st_ret = B.din("state_ret_c", [NH, HD, HD])
        tqc = B.din("tqc", [NTOK, 512]); tqs = B.din("tqs", [NTOK, 512])
        coef = B.din("coef", [1, NCORES * NH])
        y = B.dout("y_loc", [SEG + 32, D])
        o_ncp = B.dout("nconv_p", [30, DCONV]); o_nrp = B.dout("nret_p", [NH, HD, HD])
        o_ncs = B.dout("nconv_s", [30, DCONV]); o_nrs = B.dout("nret_s", [NH, HD, HD])
    if DEBUG_OUT:
        B.dout("dbg_h", [128, 4 * 94])
    if mode == "fused":
        mk = B.dout if DEBUG_OUT else B.dint
        mkL = B.dint
        mkLall = B.dint
    elif mode == "A":
        mk = B.dout
        mkL = B.dout
        mkLall = None
    else:
        mk = B.din
        mkL = None
        mkLall = B.din
    x1s = mk("x1s", [NTOK, D])
    ks = mk("ks", [NTOK, 512], BF16); vs = mk("vs", [NTOK, 512], BF16)
    x2s = B.dint("x2s", [NTOK, D]) if doB else None
    Lsend = mkL("Lsend", [128, 512]) if mkL else None
    Lall = mkLall("Lall", [NCORES * 128, 512]) if mkLall else None
    B.setup_consts()
    B.setup_mixer_consts()
    ftiles = [[(t * 1024 + b * 128, 128) for b in range(8)] for t in range(2)]
    ftiles[-1].append((SEG, SB))
    if doA and 'f1' in PHASES:
        B.ffn_phase("f1", x, lambda ti, b: (x1s, ftiles[ti][b][0], ftiles[ti][b][1]),
                    names2["ffn1"]["wg"], names2["ffn1"]["wu"], names2["ffn1"]["wd"],
                    names2["ffn1"]["pre"], names2["ffn1"]["post"], ftiles)
    blocks = [(b * 128, 128, True) for b in range(NBLK)] + [(SEG, SB, False)]
    if doA and 'kv' in PHASES:
        B.kv_phase(blocks, x1s, ks, vs, w_in, gmix, tkc, tks, Lsend)
    if doB and 'ex' in PHASES:
        B.exchange_phase(Lsend, Lall, coef, use_cc=(mode == "fused"))
    mtiles = [dict(kind="S", blocks=[(SEG, SB)], nt_ret=32)]
    for t in range(4):
        mtiles.append(dict(kind="P", blocks=[(t * 512 + b * 128, 128) for b in range(4)], nt_ret=128))
    if doB and 'mx' in PHASES:
        B.P.maxops = MX_MAXOPS
        B.mixer_phase(mtiles, x1s, x2s, ks, vs, w_in, w_out, gmix, gmixp, tqc, tqs, conv_w, conv_b, lng, lnb, gng,
                      st_conv, st_ret, Lall, coef, o_ncp, o_nrp, o_ncs, o_nrs)

    def ydst(ti, b):
        r0, pt = ftiles[ti][b]
        return (y, r0, 32 if r0 == SEG else pt)
    B.P.maxops = None
    if doB and 'f2' in PHASES:
        B.ffn_phase("f2", x2s, ydst, names2["ffn2"]["wg"], names2["ffn2"]["wu"], names2["ffn2"]["wd"],
                    names2["ffn2"]["pre"], names2["ffn2"]["post"], ftiles)
    B.top.close()
    _CACHE[("nc", mode)] = nc
    _CACHE["tables"] = _tables()
    return nc


def kernel(x_prompt, x_sample, state_conv, state_ret,
           ffn1_norm_pre, ffn1_w_gate, ffn1_w_up, ffn1_w_down, ffn1_norm_post,
           mix_norm_pre, w_in, conv_w, conv_b, conv_ln_g, conv_ln_b, ret_gn_g, w_out, mix_norm_post,
           ffn2_norm_pre, ffn2_w_gate, ffn2_w_up, ffn2_w_down, ffn2_norm_post):
    nc = build_program("fused") if not TWO_LAUNCH else None
    if "tables" not in _CACHE:
        _CACHE["tables"] = _tables()
    tabs = _CACHE["tables"]
    f32 = lambda a: np.ascontiguousarray(np.asarray(a, dtype=np.float32))
    x_prompt = f32(x_prompt); x_sample = f32(x_sample)
    shared = {
        "ident": np.eye(128, dtype=np.float32),
        "maskT": np.tile(np.triu(np.ones((128, 128), np.float32)), (1, NH)),
        "ffn1_norm_pre": f32(ffn1_norm_pre), "ffn1_w_gate": f32(ffn1_w_gate)[0], "ffn1_w_up": f32(ffn1_w_up)[0],
        "ffn1_w_down": f32(ffn1_w_down)[0], "ffn1_norm_post": f32(ffn1_norm_post),
        "ffn2_norm_pre": f32(ffn2_norm_pre), "ffn2_w_gate": f32(ffn2_w_gate)[0], "ffn2_w_up": f32(ffn2_w_up)[0],
        "ffn2_w_down": f32(ffn2_w_down)[0], "ffn2_norm_post": f32(ffn2_norm_post),
        "mix_norm_pre": f32(mix_norm_pre), "w_in": f32(w_in)[0], "conv_w": f32(conv_w)[0], "conv_b": f32(conv_b),
        "conv_ln_g": f32(conv_ln_g), "conv_ln_b": f32(conv_ln_b), "ret_gn_g": f32(ret_gn_g), "w_out": f32(w_out)[0],
        "mix_norm_post": f32(mix_norm_post),
    }
    in_maps = []
    for c in range(NCORES):
        b, seg = c // 4, c % 4
        halo = x_prompt[b, seg * SEG - 32:seg * SEG] if seg > 0 else np.zeros((32, D), np.float32)
        x_loc = np.concatenate([x_prompt[b, seg * SEG:(seg + 1) * SEG], x_sample[c], halo], axis=0)
        m = dict(shared)
        m["x_loc"] = np.ascontiguousarray(x_loc)
        m["state_conv_c"] = f32(state_conv)[0, c]
        m["state_ret_c"] = f32(state_ret)[0, c]
        m.update(tabs[c])
        in_maps.append(m)
    if not TWO_LAUNCH:
        res = run_bass_kernel_spmd(nc, in_maps, core_ids=list(range(NCORES)))
        R = res.results
    else:
        ncA = build_program("A")
        ncB = build_program("B")
        keysA = ("ident", "maskT", "ffn1_norm_pre", "ffn1_w_gate", "ffn1_w_up", "ffn1_w_down", "ffn1_norm_post",
                 "mix_norm_pre", "w_in", "x_loc", "tkc", "tks")
        resA = run_bass_kernel_spmd(ncA, [{k: m[k] for k in keysA} for m in in_maps], core_ids=list(range(NCORES)))
        RA = resA.results
        Lall = np.ascontiguousarray(np.concatenate([RA[c]["Lsend"] for c in range(NCORES)], axis=0))
        keysB = ("ident", "maskT", "ffn2_norm_pre", "ffn2_w_gate", "ffn2_w_up", "ffn2_w_down", "ffn2_norm_post",
                 "mix_norm_pre", "w_in", "conv_w", "conv_b", "conv_ln_g", "conv_ln_b", "ret_gn_g", "w_out",
                 "mix_norm_post", "state_conv_c", "state_ret_c", "tqc", "tqs", "coef")
        in_B = []
        for c in range(NCORES):
            m = {k: in_maps[c][k] for k in keysB}
            m["x1s"] = RA[c]["x1s"]; m["ks"] = RA[c]["ks"]; m["vs"] = RA[c]["vs"]; m["Lall"] = Lall
            in_B.append(m)
        res = run_bass_kernel_spmd(ncB, in_B, core_ids=list(range(NCORES)))
        R = res.results
    _CACHE["last_res"] = R
    y_prompt = np.stack([np.concatenate([R[b * 4 + s]["y_loc"][:SEG] for s in range(4)], axis=0) for b in range(2)])
    y_sample = np.stack([R[c]["y_loc"][SEG:SEG + 32] for c in range(NCORES)])
    ncp = np.stack([R[b * 4 + 3]["nconv_p"] for b in range(2)])[None]
    nrp = np.stack([R[b * 4 + 3]["nret_p"] for b in range(2)])[None]
    ncs = np.stack([R[c]["nconv_s"] for c in range(NCORES)])[None]
    nrs = np.stack([R[c]["nret_s"] for c in range(NCORES)])[None]
    return (y_prompt.astype(np.float32), y_sample.astype(np.float32), ncp.astype(np.float32),
            nrp.astype(np.float32), ncs.astype(np.float32), nrs.astype(np.float32))
```
